# Optimizing a Trainium2 kernel written in Bass

```python
import jax, jax.numpy as jnp
from jax import lax
import numpy as np

D_MODEL = 4096
BATCH = 1
SEQ = 16384
DEPTH = 2

MLA_HEADS = 16
MLA_Q_RANK = 768
MLA_KV_RANK = 512
MLA_NOPE_DIM = 128
MLA_ROPE_DIM = 64
MLA_V_DIM = 128
MLA_WIDTH = MLA_HEADS * MLA_V_DIM
ROPE_THETA = 10000.0

NSA_HEADS = 16
NSA_GROUPS = 2
NSA_HEADS_PER_GROUP = NSA_HEADS // NSA_GROUPS
NSA_HEAD_DIM = 128
NSA_WIDTH = NSA_HEADS * NSA_HEAD_DIM
NSA_BRANCHES = 3
CMP_BLOCK = 32
CMP_STRIDE = 16
SLC_BLOCK = 64
SLC_TOPK = 16
WINDOW = 512

MIX_WIDTH = MLA_WIDTH + NSA_WIDTH
IN_SPLITS = (MLA_Q_RANK, MLA_KV_RANK, MLA_ROPE_DIM, MLA_WIDTH,
             NSA_WIDTH, NSA_BRANCHES * 2 * NSA_GROUPS * NSA_HEAD_DIM, NSA_BRANCHES * NSA_HEADS, NSA_WIDTH)
IN_COLS = sum(IN_SPLITS)

Q_BLOCK = 128
LN_EPS = 1e-5
RMS_EPS = 1e-6
NEG_INF = -1e30
FORCE_BONUS = 1e4
DEEPNORM_ALPHA = (2 * DEPTH) ** 0.25
DEEPNORM_BETA = (8 * DEPTH) ** -0.25

kernel_name = 'hybrid_mla_nsa_deepnorm_adaln'


def _layer_norm(x, g, b):
    xf = x.astype(jnp.float32)
    mu = jnp.mean(xf, -1, keepdims=True)
    var = jnp.mean(jnp.square(xf - mu), -1, keepdims=True)
    y = (xf - mu) * lax.rsqrt(var + LN_EPS) * g.astype(jnp.float32) + b.astype(jnp.float32)
    return y.astype(x.dtype)


def _rms_norm(x, g):
    xf = x.astype(jnp.float32)
    y = xf * lax.rsqrt(jnp.mean(xf * xf, -1, keepdims=True) + RMS_EPS) * g.astype(jnp.float32)
    return y.astype(x.dtype)


def _rope(x, cos, sin):
    half = x.shape[-1] // 2
    xf = x.astype(jnp.float32)
    x1, x2 = xf[..., :half], xf[..., half:]
    return jnp.concatenate([x1 * cos - x2 * sin, x2 * cos + x1 * sin], -1).astype(x.dtype)


def _masked_softmax(scores, valid):
    s = jnp.where(valid, scores.astype(jnp.float32), NEG_INF)
    return jnp.where(valid, jax.nn.softmax(s, axis=-1), 0.0)


def _mla_attention(q_nope, q_rope, k_nope, k_rope, v):
    b, s, h, _ = q_nope.shape
    scale = (MLA_NOPE_DIM + MLA_ROPE_DIM) ** -0.5
    k_pos = jnp.arange(s)

    def block(i):
        q0 = i * Q_BLOCK
        qn = lax.dynamic_slice_in_dim(q_nope, q0, Q_BLOCK, 1)
        qr = lax.dynamic_slice_in_dim(q_rope, q0, Q_BLOCK, 1)
        t = q0 + jnp.arange(Q_BLOCK)
        scores = (jnp.einsum('bqhd,bkhd->bhqk', qn, k_nope)
                  + jnp.einsum('bqhd,bkd->bhqk', qr, k_rope)).astype(jnp.float32) * scale
        p = _masked_softmax(scores, k_pos[None, :] <= t[:, None])
        o = jnp.einsum('bhqk,bkhd->bqhd', p.astype(v.dtype), v)
        return o.reshape(b, Q_BLOCK, h * MLA_V_DIM)

    out = lax.map(block, jnp.arange(s // Q_BLOCK))
    return out.transpose(1, 0, 2, 3).reshape(b, s, h * MLA_V_DIM)


def _compress(k, pos_emb, w1, w2):
    b, s, g, dh = k.shape
    n_cmp = (s - CMP_BLOCK) // CMP_STRIDE + 1
    idx = jnp.arange(n_cmp)[:, None] * CMP_STRIDE + jnp.arange(CMP_BLOCK)[None, :]
    blocks = k[:, idx] + pos_emb[None, None, :, None, :]
    blocks = blocks.transpose(0, 1, 3, 2, 4).reshape(b, n_cmp, g, CMP_BLOCK * dh)
    return jax.nn.silu(blocks @ w1) @ w2


def _gather_tokens(kv_t, idx):
    return jax.vmap(jax.vmap(lambda a, i: a[i]))(kv_t, idx)


def _nsa_attention(q, k_cmp, v_cmp, k_slc, v_slc, k_win, v_win, gates):
    b, s, h, dh = q.shape
    g, hg = NSA_GROUPS, NSA_HEADS_PER_GROUP
    n_cmp = k_cmp.shape[1]
    n_slc = s // SLC_BLOCK
    n_sel = min(SLC_TOPK, n_slc)
    n_tok = n_sel * SLC_BLOCK
    scale = dh ** -0.5
    slopes = jnp.exp2(-8.0 * jnp.arange(1, h + 1, dtype=jnp.float32) / h).reshape(g, hg)
    slope_b = slopes[None, None, :, :, None]
    cmp_start = jnp.arange(n_cmp) * CMP_STRIDE
    cmp_end = cmp_start + CMP_BLOCK - 1
    slc_start = jnp.arange(n_slc) * SLC_BLOCK
    cmp_to_slc = ((cmp_start[:, None] < slc_start[None, :] + SLC_BLOCK)
                  & (cmp_end[:, None] >= slc_start[None, :])).astype(jnp.float32)
    k_slc_t = k_slc.transpose(0, 2, 1, 3)
    v_slc_t = v_slc.transpose(0, 2, 1, 3)
    pad = ((0, 0), (WINDOW, 0), (0, 0), (0, 0))
    k_win_p = jnp.pad(k_win, pad)
    v_win_p = jnp.pad(v_win, pad)
    blk_ids = jnp.arange(n_slc)
    slc_offs = jnp.arange(SLC_BLOCK)
    win_offs = jnp.arange(Q_BLOCK + WINDOW) - WINDOW

    def block(i):
        q0 = i * Q_BLOCK
        t = q0 + jnp.arange(Q_BLOCK)
        qb = lax.dynamic_slice_in_dim(q, q0, Q_BLOCK, 1).reshape(b, Q_BLOCK, g, hg, dh)
        gb = lax.dynamic_slice_in_dim(gates, q0, Q_BLOCK, 1).reshape(b, Q_BLOCK, NSA_BRANCHES, g, hg, 1)
        dist_c = (t[:, None] - cmp_end[None, :]).astype(jnp.float32)
        sc = (jnp.einsum('bqghd,bngd->bqghn', qb, k_cmp).astype(jnp.float32) * scale
              - slope_b * dist_c[None, :, None, None, :])
        p_c = _masked_softmax(sc, (dist_c >= 0)[None, :, None, None, :])
        o_c = jnp.einsum('bqghn,bngd->bqghd', p_c.astype(v_cmp.dtype), v_cmp)
        imp = jnp.einsum('bqgn,nj->bqgj', p_c.sum(3), cmp_to_slc)
        cur = t // SLC_BLOCK
        forced = ((blk_ids[None, :] == 0) | (blk_ids[None, :] == cur[:, None])
                  | (blk_ids[None, :] == cur[:, None] - 1))
        causal = slc_start[None, :] <= t[:, None]
        imp = jnp.where(causal[None, :, None, :],
                        imp + jnp.where(forced, FORCE_BONUS, 0.0)[None, :, None, :], NEG_INF)
        _, top = lax.top_k(imp, n_sel)
        tok = (top[..., None] * SLC_BLOCK + slc_offs).reshape(b, Q_BLOCK, g, n_tok)
        idx = tok.transpose(0, 2, 1, 3).reshape(b, g, Q_BLOCK * n_tok)
        ks = _gather_tokens(k_slc_t, idx).reshape(b, g, Q_BLOCK, n_tok, dh)
        vs = _gather_tokens(v_slc_t, idx).reshape(b, g, Q_BLOCK, n_tok, dh)
        dist_s = (t[None, :, None, None] - tok).astype(jnp.float32)
        ss = (jnp.einsum('bqghd,bgqtd->bqght', qb, ks).astype(jnp.float32) * scale
              - slope_b * dist_s[:, :, :, None, :])
        p_s = _masked_softmax(ss, (dist_s >= 0)[:, :, :, None, :])
        o_s = jnp.einsum('bqght,bgqtd->bqghd', p_s.astype(vs.dtype), vs)
        kw = lax.dynamic_slice_in_dim(k_win_p, q0, Q_BLOCK + WINDOW, 1)
        vw = lax.dynamic_slice_in_dim(v_win_p, q0, Q_BLOCK + WINDOW, 1)
        s_pos = q0 + win_offs
        dist_w = t[:, None] - s_pos[None, :]
        valid_w = (dist_w >= 0) & (dist_w < WINDOW) & (s_pos >= 0)[None, :]
        sw = (jnp.einsum('bqghd,bkgd->bqghk', qb, kw).astype(jnp.float32) * scale
              - slope_b * dist_w.astype(jnp.float32)[None, :, None, None, :])
        p_w = _masked_softmax(sw, valid_w[None, :, None, None, :])
        o_w = jnp.einsum('bqghk,bkgd->bqghd', p_w.astype(vw.dtype), vw)
        o = gb[:, :, 0] * o_c + gb[:, :, 1] * o_s + gb[:, :, 2] * o_w
        return o.reshape(b, Q_BLOCK, h * dh)

    out = lax.map(block, jnp.arange(s // Q_BLOCK))
    return out.transpose(1, 0, 2, 3).reshape(b, s, h * dh)


def _hybrid_layer(x, c, cos, sin, w_ada, b_ada, w_in, q_norm, w_q_up, kv_norm, w_kv_up,
                  cmp_pos, w_cmp1, w_cmp2, w_out, ln_g, ln_b):
    b, s, _ = x.shape
    shift, scale, gate = jnp.split(jax.nn.silu(c) @ w_ada + b_ada, 3, axis=-1)
    h = x * (1.0 + scale[:, None, :]) + shift[:, None, :]
    offsets = np.cumsum(IN_SPLITS)[:-1].tolist()
    q_lat, kv_lat, k_rope, z_mla, q_nsa, kv_nsa, g_nsa, z_nsa = jnp.split(h @ w_in, offsets, axis=-1)

    q = (_rms_norm(q_lat, q_norm) @ w_q_up).reshape(b, s, MLA_HEADS, MLA_NOPE_DIM + MLA_ROPE_DIM)
    q_nope = q[..., :MLA_NOPE_DIM]
    q_rope = _rope(q[..., MLA_NOPE_DIM:], cos[:, :, None, :], sin[:, :, None, :])
    kv = (_rms_norm(kv_lat, kv_norm) @ w_kv_up).reshape(b, s, MLA_HEADS, MLA_NOPE_DIM + MLA_V_DIM)
    k_nope, v = kv[..., :MLA_NOPE_DIM], kv[..., MLA_NOPE_DIM:]
    k_rope = _rope(k_rope, cos, sin)
    o_mla = _mla_attention(q_nope, q_rope, k_nope, k_rope, v) * jax.nn.silu(z_mla)

    kvn = kv_nsa.reshape(b, s, NSA_BRANCHES, 2, NSA_GROUPS, NSA_HEAD_DIM)
    k_cmp = _compress(kvn[:, :, 0, 0], cmp_pos[0], w_cmp1[0], w_cmp2[0])
    v_cmp = _compress(kvn[:, :, 0, 1], cmp_pos[1], w_cmp1[1], w_cmp2[1])
    gates = jax.nn.sigmoid(g_nsa).reshape(b, s, NSA_BRANCHES, NSA_HEADS)
    o_nsa = _nsa_attention(q_nsa.reshape(b, s, NSA_HEADS, NSA_HEAD_DIM), k_cmp, v_cmp,
                           kvn[:, :, 1, 0], kvn[:, :, 1, 1], kvn[:, :, 2, 0], kvn[:, :, 2, 1],
                           gates) * jax.nn.silu(z_nsa)

    y = jnp.concatenate([o_mla, o_nsa], axis=-1) @ w_out
    return _layer_norm(DEEPNORM_ALPHA * x + gate[:, None, :] * y, ln_g, ln_b)


def setup_inputs(seed: int = 0) -> dict:
    key = jax.random.key(seed)
    ks = jax.random.split(key, 18)
    f32 = jnp.float32
    nrm = lambda k, shape, sc: jax.random.normal(k, shape, f32) * sc
    x = nrm(ks[0], (BATCH, SEQ, D_MODEL), 1.0)
    c = nrm(ks[1], (BATCH, D_MODEL), 1.0)
    positions = (jnp.arange(SEQ, dtype=jnp.int32)[None, :]
                 + jax.random.randint(ks[2], (BATCH, 1), 0, 1024, dtype=jnp.int32))
    w_ada = nrm(ks[3], (DEPTH, D_MODEL, 3 * D_MODEL), 0.5 * D_MODEL ** -0.5)
    b_ada = nrm(ks[4], (DEPTH, 3 * D_MODEL), 0.01)
    w_in = nrm(ks[5], (DEPTH, D_MODEL, IN_COLS), D_MODEL ** -0.5)
    mla_q_norm = 1.0 + nrm(ks[6], (DEPTH, MLA_Q_RANK), 0.01)
    w_q_up = nrm(ks[7], (DEPTH, MLA_Q_RANK, MLA_HEADS * (MLA_NOPE_DIM + MLA_ROPE_DIM)), MLA_Q_RANK ** -0.5)
    mla_kv_norm = 1.0 + nrm(ks[8], (DEPTH, MLA_KV_RANK), 0.01)
    w_kv_up = nrm(ks[9], (DEPTH, MLA_KV_RANK, MLA_HEADS * (MLA_NOPE_DIM + MLA_V_DIM)), MLA_KV_RANK ** -0.5)
    cmp_pos = nrm(ks[10], (DEPTH, 2, CMP_BLOCK, NSA_HEAD_DIM), 0.1)
    w_cmp1 = nrm(ks[11], (DEPTH, 2, CMP_BLOCK * NSA_HEAD_DIM, NSA_HEAD_DIM), (CMP_BLOCK * NSA_HEAD_DIM) ** -0.5)
    w_cmp2 = nrm(ks[12], (DEPTH, 2, NSA_HEAD_DIM, NSA_HEAD_DIM), NSA_HEAD_DIM ** -0.5)
    w_out = nrm(ks[13], (DEPTH, MIX_WIDTH, D_MODEL), DEEPNORM_BETA * MIX_WIDTH ** -0.5)
    ln_g = 1.0 + nrm(ks[14], (DEPTH, D_MODEL), 0.01)
    ln_b = nrm(ks[15], (DEPTH, D_MODEL), 0.01)
    return {'x': x, 'c': c, 'positions': positions, 'w_ada': w_ada, 'b_ada': b_ada, 'w_in': w_in,
            'mla_q_norm': mla_q_norm, 'w_q_up': w_q_up, 'mla_kv_norm': mla_kv_norm, 'w_kv_up': w_kv_up,
            'cmp_pos': cmp_pos, 'w_cmp1': w_cmp1, 'w_cmp2': w_cmp2, 'w_out': w_out,
            'ln_g': ln_g, 'ln_b': ln_b}


def reference(x, c, positions, w_ada, b_ada, w_in, mla_q_norm, w_q_up, mla_kv_norm, w_kv_up,
              cmp_pos, w_cmp1, w_cmp2, w_out, ln_g, ln_b):
    inv_freq = ROPE_THETA ** (-jnp.arange(0, MLA_ROPE_DIM, 2, dtype=jnp.float32) / MLA_ROPE_DIM)
    ang = positions.astype(jnp.float32)[..., None] * inv_freq
    cos, sin = jnp.cos(ang), jnp.sin(ang)
    for l in range(DEPTH):
        x = _hybrid_layer(x, c, cos, sin, w_ada[l], b_ada[l], w_in[l], mla_q_norm[l], w_q_up[l],
                          mla_kv_norm[l], w_kv_up[l], cmp_pos[l], w_cmp1[l], w_cmp2[l], w_out[l],
                          ln_g[l], ln_b[l])
    return x
```

```python
import numpy as np
import ml_dtypes
from contextlib import ExitStack
import concourse.bass as bass
import concourse.mybir as mybir
from concourse.bass_utils import run_bass_kernel_spmd

F32 = mybir.dt.float32
BF16 = mybir.dt.bfloat16
AF = mybir.ActivationFunctionType
ALU = mybir.AluOpType
AX = mybir.AxisListType
NPBF = ml_dtypes.bfloat16

ENGS = ["pe", "act", "dve", "pool", "sp"]


class Prog:
    def __init__(self, nc, st):
        self.nc = nc
        self.st = st
        self.ops = {e: [] for e in ENGS}
        self.cnt = {e: 0 for e in ENGS}
        self.esem = {e: st.enter_context(nc.semaphore("c_" + e)) for e in ENGS if e != "sp"}
        self.dsem = {}
        self.dcnt = {}
        self.seen = {e: {} for e in ENGS}
        self.buf = {}

    def sb(self, name, shape, dt):
        return self.st.enter_context(self.nc.sbuf_tensor(name, list(shape), dt))

    def ps(self, name, shape, dt):
        return self.st.enter_context(self.nc.psum_tensor(name, list(shape), dt))

    def _dsem(self, name):
        if name not in self.dsem:
            self.dsem[name] = self.st.enter_context(self.nc.semaphore("d_" + name))
            self.dcnt[name] = 0
        return self.dsem[name]

    def _deps(self, eng, reads, writes):
        need = {}

        def add(tok):
            if tok is None:
                return
            k, v = tok
            if need.get(k, 0) < v:
                need[k] = v

        for b in reads:
            s = self.buf.setdefault(b, {"w": None, "r": []})
            add(s["w"])
        for b in writes:
            s = self.buf.setdefault(b, {"w": None, "r": []})
            add(s["w"])
            for t in s["r"]:
                add(t)
        waits = []
        for k, v in need.items():
            if k[0] == "d":
                v = 16 * self.dcnt[k[1]]
            if k == ("e", eng) and eng == "pe":
                continue
            if self.seen[eng].get(k, 0) >= v:
                continue
            self.seen[eng][k] = v
            waits.append((k, v))
        return waits

    def _commit(self, tok, reads, writes):
        for b in reads:
            self.buf[b]["r"].append(tok)
        for b in writes:
            self.buf[b]["w"] = tok
            self.buf[b]["r"] = []

    def op(self, eng, fn, reads=(), writes=()):
        waits = self._deps(eng, reads, writes)
        self.cnt[eng] += 1
        tok = (("e", eng), self.cnt[eng])
        self.ops[eng].append((waits, fn, (self.esem[eng], 1)))
        self._commit(tok, reads, writes)

    def dma(self, q, sem, fn, reads=(), writes=()):
        waits = self._deps(q, reads, writes)
        s = self._dsem(sem)
        self.dcnt[sem] += 1
        tok = (("d", sem), 16 * self.dcnt[sem])
        self.ops[q].append((waits, fn, (s, 16)))
        self._commit(tok, reads, writes)

    def wait_all(self, eng, bufs):
        waits = self._deps(eng, (), bufs)
        self.ops[eng].append((waits, None, None))

    def _sem_of(self, k):
        return self.esem[k[1]] if k[0] == "e" else self.dsem[k[1]]

    def emit(self):
        nc = self.nc
        with nc.Block() as block:
            def run(eobj, lst):
                for waits, fn, inc in lst:
                    for k, v in waits:
                        eobj.wait_ge(self._sem_of(k), v)
                    if fn is not None:
                        ins = fn(eobj)
                        ins.then_inc(inc[0], inc[1])

            @block.tensor
            def _(e):
                run(e, self.ops["pe"])

            @block.scalar
            def _(e):
                run(e, self.ops["act"])

            @block.vector
            def _(e):
                run(e, self.ops["dve"])

            @block.gpsimd
            def _(e):
                run(e, self.ops["pool"])

            @block.sync
            def _(e):
                run(e, self.ops["sp"])


def _mm(P, out, lhsT, rhs, start, stop, reads, writes, skip=False):
    P.op("pe", lambda e, o=out, l=lhsT, r=rhs, s=start, t=stop, k=skip: e.matmul(o, l, r, start=s, stop=t, skip_group_check=k),
         reads=reads, writes=writes)


def _tr(P, out, in_, ident, reads, writes):
    P.op("pe", lambda e, o=out, i=in_, d=ident: e.transpose(o, i, d), reads=reads, writes=writes)


def _act(P, out, in_, func, reads, writes, bias=None, scale=None, accum_out=None, eng="act"):
    kw = {}
    if bias is not None:
        kw["bias"] = bias
    if scale is not None:
        kw["scale"] = scale
    if accum_out is not None:
        kw["accum_out"] = accum_out
    P.op("act", lambda e, o=out, i=in_, f=func, k=kw: e.activation(out=o, in_=i, func=f, **k),
         reads=reads, writes=writes)


def _copy(P, eng, out, in_, reads, writes):
    if eng == "act":
        P.op("act", lambda e, o=out, i=in_: e.copy(o, i), reads=reads, writes=writes)
    else:
        P.op(eng, lambda e, o=out, i=in_: e.tensor_copy(o, i), reads=reads, writes=writes)


def _ts(P, eng, out, in0, s1, s2, op0, op1, reads, writes, accum_out=None):
    kw = {}
    if accum_out is not None:
        kw["accum_out"] = accum_out
    if op1 is None:
        P.op(eng, lambda e, o=out, i=in0, a=s1, p0=op0: e.tensor_scalar(out=o, in0=i, scalar1=a, scalar2=None, op0=p0),
             reads=reads, writes=writes)
    else:
        P.op(eng, lambda e, o=out, i=in0, a=s1, b=s2, p0=op0, p1=op1, k=kw: e.tensor_scalar(out=o, in0=i, scalar1=a, scalar2=b, op0=p0, op1=p1, **k),
             reads=reads, writes=writes)


def _tt(P, eng, out, in0, in1, op, reads, writes):
    P.op(eng, lambda e, o=out, a=in0, b=in1, p=op: e.tensor_tensor(out=o, in0=a, in1=b, op=p), reads=reads, writes=writes)


def _stt(P, eng, out, in0, scalar, in1, op0, op1, reads, writes):
    P.op(eng, lambda e, o=out, a=in0, s=scalar, b=in1, p0=op0, p1=op1: e.scalar_tensor_tensor(out=o, in0=a, scalar=s, in1=b, op0=p0, op1=p1),
         reads=reads, writes=writes)


def _dma(P, q, sem, out, in_, reads, writes):
    P.dma(q, sem, lambda e, o=out, i=in_: e.dma_start(out=o, in_=i), reads=reads, writes=writes)


class Ring:
    def __init__(self, items):
        self.items = list(items)
        self.i = 0

    def next(self):
        it = self.items[self.i % len(self.items)]
        self.i += 1
        return it


import math

I32 = mybir.dt.int32
NCORE = 8
SEQ = 16384
DM = 4096
TSH = SEQ // NCORE
TC = 512
MLA_SCALE = 192 ** -0.5
NSA_SCALE = 128 ** -0.5
ALPHA = 4 ** 0.25
TWO_PI = 2.0 * math.pi
CW1 = 6.28125
CW2 = float(np.float32(TWO_PI - CW1).view(np.uint32) & np.uint32(0xFFFFF000)) if False else None


def _cw_consts():
    c1 = np.float32(6.28125)
    rem = TWO_PI - float(c1)
    c2 = np.float32(rem)
    c2 = np.frombuffer(np.uint32(np.frombuffer(c2.tobytes(), np.uint32)[0] & 0xFFFFF000).tobytes(), np.float32)[0]
    c3 = np.float32(rem - float(c2))
    return float(c1), float(c2), float(c3)


CW1, CW2, CW3 = _cw_consts()
MAGIC = 12582912.0
PI_LO = float(np.nextafter(np.float32(math.pi), np.float32(0)))


def finish(P):
    waits = []
    for name, s in P.dsem.items():
        v = 16 * P.dcnt[name]
        if P.seen["sp"].get(("d", name), 0) < v:
            waits.append((("d", name), v))
    P.ops["sp"].append((waits, None, None))
    P.emit()


def build_L0():
    nc = bass.Bass("TRN2", target_bir_lowering=False)
    cT = nc.dram_tensor("cT", [128, 32], F32, kind="ExternalInput").ap()
    wada = nc.dram_tensor("wada", [2, DM, 1536], F32, kind="ExternalInput").ap()
    bada = nc.dram_tensor("bada", [128, 24], F32, kind="ExternalInput").ap()
    out = nc.dram_tensor("mod", [128, 24], F32, kind="ExternalOutput").ap()
    with ExitStack() as st:
        P = Prog(nc, st)
        c_sb = P.sb("c_sb", [128, 32], F32)
        s_sb = P.sb("s_sb", [128, 32], F32)
        b_sb = P.sb("b_sb", [128, 24], F32)
        o_sb = P.sb("o_sb", [128, 24], F32)
        wsl = [P.sb(f"w{i}", [128, 32, 128], F32) for i in range(3)]
        acc = P.ps("acc", [128, 24], F32)
        _dma(P, "sp", "c", c_sb[:], cT, [], ["c_sb"])
        _dma(P, "sp", "c", b_sb[:], bada, [], ["b_sb"])
        _act(P, s_sb[:], c_sb[:], AF.Silu, ["c_sb"], ["s_sb"])
        i = 0
        for l in range(2):
            wl = wada[l].rearrange("(k p) c -> p k c", p=128)
            for j in range(12):
                sl = i % 3
                _dma(P, "sp" if i % 2 == 0 else "act", f"w{sl}", wsl[sl][:], wl[:, :, j * 128:(j + 1) * 128], [], [f"w{sl}"])
                col = l * 12 + j
                for k in range(32):
                    _mm(P, acc[:, col:col + 1], wsl[sl][:, k, :], s_sb[:, k:k + 1], k == 0, k == 31,
                        [f"w{sl}", "s_sb"], ["acc"])
                i += 1
        _tt(P, "dve", o_sb[:], acc[:], b_sb[:], ALU.add, ["acc", "b_sb"], ["o_sb"])
        _dma(P, "sp", "o", out, o_sb[:], ["o_sb"], [])
        finish(P)
    return nc


WF_COLS = 4416
WT_COLS = 4656


def build_L1(nchunks=None, parts='CDE'):
    nc = bass.Bass("TRN2", target_bir_lowering=False)
    T = TSH
    din = lambda n, s, d=F32: nc.dram_tensor(n, list(s), d, kind="ExternalInput").ap()
    dout = lambda n, s, d=BF16: nc.dram_tensor(n, list(s), d, kind="ExternalOutput").ap()
    xT = din("xT", [DM, T])
    mod = din("mod", [128, 64])
    wf = din("wf", [DM, WF_COLS])
    wt = din("wt", [DM, WT_COLS])
    wq = din("wq", [768, 3072])
    wkv = din("wkv", [512, 4096])
    qg = din("qg", [128, 6])
    kvg = din("kvg", [128, 4])
    pos = din("pos", [1, T], I32)
    invf = din("invf", [64, 1])
    rmat = din("rmat", [64, 64])
    o_qn = dout("o_qn", [16, 128, T])
    o_qr = dout("o_qr", [16, 64, T])
    o_kn = dout("o_kn", [16, 128, T])
    o_kr = dout("o_kr", [64, T])
    o_v = dout("o_v", [T, 2048])
    o_z = dout("o_z", [T, 4096])
    o_qs = dout("o_qs", [16, 128, T])
    o_kp = dout("o_kp", [8, 128, T])
    o_vp = dout("o_vp", [T, 512])
    o_g = dout("o_g", [T, 48], F32)
    with ExitStack() as st:
        P = Prog(nc, st)
        hT = P.sb("hT", [128, 32, TC], BF16)
        Wsl = [P.sb(f"W{i}", [128, 32, 512], BF16) for i in range(2)]
        Wu = [P.sb(f"Wu{i}", [128, 6, 1024], BF16) for i in range(2)]
        lat32 = P.sb("lat32", [128, 6, TC], F32)
        qnT = P.sb("qnT", [128, 6, TC], BF16)
        kvnT = P.sb("kvnT", [128, 4, TC], BF16)
        sq = [P.sb(f"sq{i}", [128, TC], BF16) for i in range(2)]
        rstd = P.sb("rstd", [128, TC], F32)
        rt = P.sb("rt", [128, TC], F32)
        xs = [P.sb(f"xs{i}", [128, TC], F32) for i in range(4)]
        stg = [P.sb(f"stg{i}", [128, TC], BF16) for i in range(6)]
        stg32 = P.sb("stg32", [128, 4, 48], F32)
        modsb = P.sb("modsb", [128, 64], F32)
        sc1 = P.sb("sc1", [128, 32], F32)
        qg_sb = P.sb("qg_sb", [128, 6], F32)
        kvg_sb = P.sb("kvg_sb", [128, 4], F32)
        invf_sb = P.sb("invf_sb", [64, 1], F32)
        rm_sb = P.sb("rm_sb", [64, 64], BF16)
        ones = P.sb("ones", [128, 128], BF16)
        posi = P.sb("posi", [64, TC], I32)
        ang = P.sb("ang", [64, TC], F32)
        kf = P.sb("kf", [64, TC], F32)
        rr = P.sb("rr", [64, TC], F32)
        sinT = P.sb("sinT", [64, TC], F32)
        cosT = P.sb("cosT", [64, TC], F32)
        xb = [P.sb(f"xb{i}", [64, TC], BF16) for i in range(2)]
        t1 = P.sb("t1", [64, TC], F32)
        t2 = P.sb("t2", [64, TC], F32)
        S = [P.ps(f"S{i}", [128, 512], F32) for i in range(4)]
        SS = P.ps("SS", [128, 512], F32)
        SW = [P.ps(f"SW{i}", [64, 512], F32) for i in range(2)]

        Sr = Ring(range(4))
        xr = Ring(range(4))
        sr = Ring(range(6))
        Wr = Ring(range(2))
        Wur = Ring(range(2))
        sqr = Ring(range(2))
        xbr = Ring(range(2))
        swr = Ring(range(2))
        stq = Ring(["sp", "act"])

        _dma(P, "sp", "c", modsb[:], mod, [], ["modsb"])
        _dma(P, "sp", "c", qg_sb[:], qg, [], ["qg_sb"])
        _dma(P, "sp", "c", kvg_sb[:], kvg, [], ["kvg_sb"])
        _dma(P, "sp", "c", invf_sb[:], invf, [], ["invf_sb"])
        _dma(P, "pool", "cp", rm_sb[:], rmat, [], ["rm_sb"])
        P.op("dve", lambda e: e.memset(ones[:], 1.0), writes=["ones"])
        _ts(P, "dve", sc1[:], modsb[:, 32:64], 1.0, None, ALU.add, None, ["modsb"], ["sc1"])

        wf3 = wf.rearrange("(k p) c -> p k c", p=128)
        wt3 = wt.rearrange("(k p) c -> p k c", p=128)
        wq3 = wq.rearrange("(k p) c -> p k c", p=128)
        wkv3 = wkv.rearrange("(k p) c -> p k c", p=128)

        def load_W(src3, c0, n):
            sl = Wr.next()
            for kk in range(0, 32, 8):
                _dma(P, "pool", f"W{sl}", Wsl[sl][:, kk:kk + 8, 0:n], src3[:, kk:kk + 8, c0:c0 + n], [], [f"W{sl}"])
            return sl

        def load_Wu(src3, kc, c0, n):
            sl = Wur.next()
            _dma(P, "pool", f"Wu{sl}", Wu[sl][:, 0:kc, 0:n], src3[:, 0:kc, c0:c0 + n], [], [f"Wu{sl}"])
            return sl

        def store(dst, src_ap, srcname):
            _dma(P, stq.next(), "st_" + srcname, dst, src_ap, [srcname], [])

        def rope_finish(xbi, dst, tsl):
            swi = swr.next()
            _mm(P, SW[swi][:], rm_sb[:], xb[xbi][:], True, True, ["rm_sb", f"xb{xbi}"], [f"SW{swi}"])
            _tt(P, "dve", t1[:], SW[swi][:], sinT[:], ALU.mult, [f"SW{swi}", "sinT"], ["t1"])
            _tt(P, "dve", t2[:], xb[xbi][:], cosT[:], ALU.mult, [f"xb{xbi}", "cosT"], ["t2"])
            si = sr.next()
            _tt(P, "dve", stg[si][0:64, :], t1[:], t2[:], ALU.add, ["t1", "t2"], [f"stg{si}"])
            store(dst, stg[si][0:64, :], f"stg{si}")

        for tc in range(nchunks or (T // TC)):
            tsl = slice(tc * TC, (tc + 1) * TC)
            for k in range(32):
                xi = xr.next()
                _dma(P, "sp", f"xs{xi}", xs[xi][:], xT[k * 128:(k + 1) * 128, tsl], [], [f"xs{xi}"])
                _act(P, hT[:, k, :], xs[xi][:], AF.Identity, [f"xs{xi}", "sc1", "modsb"], ["hT"],
                     bias=modsb[:, k:k + 1], scale=sc1[:, k:k + 1])
            _dma(P, "sp", "posi", posi[:], pos[:, tsl].broadcast_to([64, TC]), [], ["posi"])
            _copy(P, "dve", ang[:], posi[:], ["posi"], ["ang"])
            _ts(P, "dve", ang[:], ang[:], invf_sb[:, 0:1], None, ALU.mult, None, ["ang", "invf_sb"], ["ang"])
            _ts(P, "dve", kf[:], ang[:], 1.0 / TWO_PI, MAGIC, ALU.mult, ALU.add, ["ang"], ["kf"])
            _ts(P, "dve", kf[:], kf[:], -MAGIC, None, ALU.add, None, ["kf"], ["kf"])
            _stt(P, "dve", rr[:], kf[:], -CW1, ang[:], ALU.mult, ALU.add, ["kf", "ang"], ["rr"])
            _stt(P, "dve", rr[:], kf[:], -CW2, rr[:], ALU.mult, ALU.add, ["kf", "rr"], ["rr"])
            _stt(P, "dve", rr[:], kf[:], -CW3, rr[:], ALU.mult, ALU.add, ["kf", "rr"], ["rr"])
            _ts(P, "dve", rr[:], rr[:], PI_LO, -PI_LO, ALU.min, ALU.max, ["rr"], ["rr"])
            _act(P, sinT[:], rr[:], AF.Sin, ["rr"], ["sinT"])
            _act(P, cosT[:], rr[:], AF.Sin, ["rr"], ["cosT"], scale=0.5)
            _tt(P, "dve", cosT[:], cosT[:], cosT[:], ALU.mult, ["cosT"], ["cosT"])
            _ts(P, "dve", cosT[:], cosT[:], -2.0, 1.0, ALU.mult, ALU.add, ["cosT"], ["cosT"])

            blocks = []
            for j in range(6):
                blocks.append((j * 128, 128, "qlat", j))
            for j in range(4):
                blocks.append((768 + j * 128, 128, "kvlat", j))
            for h in range(16):
                blocks.append((1280 + h * 128, 128, "qnsa", h))
            for i in range(8):
                blocks.append((3328 + i * 128, 128, "kpart", i))
            blocks.append((4352, 64, "krope", 0))
            deferred = []
            cur_g = -1
            wsl_i = None
            import os as _os
            for (c0, w, kind, idx) in (blocks[int(_os.environ.get('L1_B0', 0)):int(_os.environ.get('L1_B1', 999))] if 'C' in parts else []):
                g = c0 // 512
                if g != cur_g:
                    n = min(512, WF_COLS - g * 512)
                    wsl_i = load_W(wf3, g * 512, n)
                    cur_g = g
                lc = c0 - g * 512
                si_ = Sr.next()
                Sp = S[si_]
                for k in range(32):
                    _mm(P, Sp[0:w, :], Wsl[wsl_i][:, k, lc:lc + w], hT[:, k, :], k == 0, k == 31,
                        [f"W{wsl_i}", "hT"], [f"S{si_}"])
                for d in deferred:
                    d()
                deferred = []
                if _os.environ.get('L1_NOPOST'):
                    pass
                elif kind in ("qlat", "kvlat"):
                    nj = 6 if kind == "qlat" else 4
                    dim = 768.0 if kind == "qlat" else 512.0
                    gsb = qg_sb if kind == "qlat" else kvg_sb
                    dstT = qnT if kind == "qlat" else kvnT
                    dname = "qnT" if kind == "qlat" else "kvnT"
                    _copy(P, "dve", lat32[:, idx, :], Sp[:], [f"S{si_}"], ["lat32"])
                    qi = sqr.next()
                    _act(P, sq[qi][:], lat32[:, idx, :], AF.Square, ["lat32"], [f"sq{qi}"])

                    def ssmm(qi=qi, idx=idx, nj=nj):
                        _mm(P, SS[:], ones[:], sq[qi][:], idx == 0, idx == nj - 1, ["ones", f"sq{qi}"], ["SS"])
                    deferred.append(ssmm)
                    if idx == nj - 1:
                        def norm(nj=nj, dim=dim, gsb=gsb, dstT=dstT, dname=dname):
                            _act(P, rt[:], SS[:], AF.Sqrt, ["SS"], ["rt"], scale=1.0 / dim, bias=1e-6)
                            P.op("dve", lambda e: e.reciprocal(out=rstd[:], in_=rt[:]), reads=["rt"], writes=["rstd"])
                            for j in range(nj):
                                _stt(P, "dve", dstT[:, j, :], lat32[:, j, :], gsb[:, j:j + 1], rstd[:], ALU.mult, ALU.mult,
                                     ["lat32", "rstd", "qg_sb", "kvg_sb"], [dname])
                        deferred.append(norm)
                elif kind == "qnsa":
                    si = sr.next()
                    _v = _os.environ.get('L1_VAR', '')
                    if 'dve' in _v:
                        _ts(P, "dve", stg[si][:], Sp[:], NSA_SCALE, None, ALU.mult, None, [f"S{si_}"], [f"stg{si}"])
                    else:
                        _act(P, stg[si][:], Sp[:], AF.Identity, [f"S{si_}"], [f"stg{si}"], scale=NSA_SCALE)
                    if 'nostore' not in _v:
                        store(o_qs[idx][:, tsl], stg[si][:], f"stg{si}")
                elif kind == "kpart":
                    si = sr.next()
                    _copy(P, "dve", stg[si][:], Sp[:], [f"S{si_}"], [f"stg{si}"])
                    store(o_kp[idx][:, tsl], stg[si][:], f"stg{si}")
                elif kind == "krope":
                    xi = xbr.next()
                    _copy(P, "dve", xb[xi][:], Sp[0:64, :], [f"S{si_}"], [f"xb{xi}"])
                    deferred.append(lambda xi=xi: rope_finish(xi, o_kr[:, tsl], tsl))
            for d in deferred:
                d()
            deferred = []

            for hg in range(4 if 'D' in parts else 0):
                wi = load_Wu(wq3, 6, hg * 768, 768)
                for hh in range(4):
                    h = hg * 4 + hh
                    si_ = Sr.next()
                    for j in range(6):
                        _mm(P, S[si_][:], Wu[wi][:, j, hh * 192:hh * 192 + 128], qnT[:, j, :], j == 0, j == 5,
                            [f"Wu{wi}", "qnT"], [f"S{si_}"])
                    si2 = Sr.next()
                    for j in range(6):
                        _mm(P, S[si2][0:64, :], Wu[wi][:, j, hh * 192 + 128:hh * 192 + 192], qnT[:, j, :], j == 0, j == 5,
                            [f"Wu{wi}", "qnT"], [f"S{si2}"])
                    for d in deferred:
                        d()
                    deferred = []
                    si = sr.next()
                    _act(P, stg[si][:], S[si_][:], AF.Identity, [f"S{si_}"], [f"stg{si}"], scale=MLA_SCALE)
                    store(o_qn[h][:, tsl], stg[si][:], f"stg{si}")
                    xi = xbr.next()
                    _act(P, xb[xi][:], S[si2][0:64, :], AF.Identity, [f"S{si2}"], [f"xb{xi}"], scale=MLA_SCALE)
                    deferred.append(lambda xi=xi, h=h: rope_finish(xi, o_qr[h][:, tsl], tsl))
            for d in deferred:
                d()
            deferred = []
            for hg in range(2 if 'D' in parts else 0):
                wi = load_Wu(wkv3, 4, hg * 1024, 1024)
                for hh in range(8):
                    h = hg * 8 + hh
                    si_ = Sr.next()
                    for j in range(4):
                        _mm(P, S[si_][:], Wu[wi][:, j, hh * 128:(hh + 1) * 128], kvnT[:, j, :], j == 0, j == 3,
                            [f"Wu{wi}", "kvnT"], [f"S{si_}"])
                    si = sr.next()
                    _copy(P, "dve", stg[si][:], S[si_][:], [f"S{si_}"], [f"stg{si}"])
                    store(o_kn[h][:, tsl], stg[si][:], f"stg{si}")
            for vg in range(2 if 'D' in parts else 0):
                wi = load_Wu(wkv3, 4, 2048 + vg * 1024, 1024)
                for half in range(2):
                    for s in range(4):
                        si_ = Sr.next()
                        for j in range(4):
                            _mm(P, S[si_][:], kvnT[:, j, s * 128:(s + 1) * 128], Wu[wi][:, j, half * 512:(half + 1) * 512],
                                j == 0, j == 3, [f"Wu{wi}", "kvnT"], [f"S{si_}"])
                        si = sr.next()
                        if s % 2 == 0:
                            _copy(P, "dve", stg[si][:], S[si_][:], [f"S{si_}"], [f"stg{si}"])
                        else:
                            _copy(P, "act", stg[si][:], S[si_][:], [f"S{si_}"], [f"stg{si}"])
                        c0 = vg * 1024 + half * 512
                        store(o_v[tc * TC + s * 128: tc * TC + (s + 1) * 128, c0:c0 + 512], stg[si][:], f"stg{si}")

            ng = (WT_COLS + 511) // 512 if 'E' in parts else 0
            for g in range(ng):
                n = min(512, WT_COLS - g * 512)
                wi = load_W(wt3, g * 512, n)
                for s in range(4):
                    si_ = Sr.next()
                    for k in range(32):
                        _mm(P, S[si_][:, 0:n], hT[:, k, s * 128:(s + 1) * 128], Wsl[wi][:, k, 0:n], k == 0, k == 31,
                            [f"W{wi}", "hT"], [f"S{si_}"])
                    r0 = tc * TC + s * 128
                    if g < 8:
                        si = sr.next()
                        _act(P, stg[si][:], S[si_][:], AF.Silu, [f"S{si_}"], [f"stg{si}"])
                        store(o_z[r0:r0 + 128, g * 512:(g + 1) * 512], stg[si][:], f"stg{si}")
                    elif g == 8:
                        si = sr.next()
                        _copy(P, "dve", stg[si][:], S[si_][:], [f"S{si_}"], [f"stg{si}"])
                        store(o_vp[r0:r0 + 128, :], stg[si][:], f"stg{si}")
                    else:
                        _act(P, stg32[:, s, :], S[si_][:, 0:48], AF.Sigmoid, [f"S{si_}"], [f"stg32_{s}"])
                        _dma(P, "sp", f"st32_{s}", o_g[r0:r0 + 128, :], stg32[:, s, :], [f"stg32_{s}"], [])
        finish(P)
    return nc


def build_L3(nchunks=None):
    nc = bass.Bass("TRN2", target_bir_lowering=False)
    T = TSH
    din = lambda n, s, d=F32: nc.dram_tensor(n, list(s), d, kind="ExternalInput").ap()
    ogT = din("ogT", [DM, T], BF16)
    xT = din("xT", [DM, T])
    wo = din("wo", [DM, DM])
    vec = din("vec", [128, 96])
    o_x = nc.dram_tensor("o_x", [DM, T], F32, kind="ExternalOutput").ap()
    with ExitStack() as st:
        P = Prog(nc, st)
        og = P.sb("og", [128, 32, TC], BF16)
        Wsl = [P.sb(f"W{i}", [128, 32, 512], BF16) for i in range(2)]
        r32 = P.sb("r32", [128, 32, TC], F32)
        xs = [P.sb(f"xs{i}", [128, TC], F32) for i in range(3)]
        rb = [P.sb(f"rb{i}", [128, TC], BF16) for i in range(2)]
        sqb = [P.sb(f"sqb{i}", [128, TC], BF16) for i in range(2)]
        vsb = P.sb("vsb", [128, 96], F32)
        ones = P.sb("ones", [128, 128], BF16)
        mean = P.sb("mean", [128, TC], F32)
        var = P.sb("var", [128, TC], F32)
        rstd = P.sb("rstd", [128, TC], F32)
        tmp = [P.sb(f"tmp{i}", [128, TC], F32) for i in range(2)]
        ost = [P.sb(f"ost{i}", [128, TC], F32) for i in range(3)]
        S = [P.ps(f"S{i}", [128, 512], F32) for i in range(4)]
        SUM = P.ps("SUM", [128, 512], F32)
        SQ = P.ps("SQ", [128, 512], F32)
        Sr, xr, Wr, rbr, tr, orr = Ring(range(4)), Ring(range(3)), Ring(range(2)), Ring(range(2)), Ring(range(2)), Ring(range(3))
        stq = Ring(["sp", "act"])
        _dma(P, "sp", "c", vsb[:], vec, [], ["vsb"])
        P.op("dve", lambda e: e.memset(ones[:], 1.0), writes=["ones"])
        wo3 = wo.rearrange("(k p) c -> p k c", p=128)
        og3 = ogT.rearrange("(k p) t -> p k t", p=128)
        for tc in range(nchunks or (T // TC)):
            tsl = slice(tc * TC, (tc + 1) * TC)
            for kk in range(0, 32, 8):
                _dma(P, "sp", "og", og[:, kk:kk + 8, :], og3[:, kk:kk + 8, tsl], [], ["og"])
            deferred = []
            wi = None
            for cb in range(32):
                if cb % 4 == 0:
                    wi = Wr.next()
                    for kk in range(0, 32, 8):
                        _dma(P, "pool", f"W{wi}", Wsl[wi][:, kk:kk + 8, :], wo3[:, kk:kk + 8, cb * 128:cb * 128 + 512], [], [f"W{wi}"])
                xi = xr.next()
                _dma(P, "sp", f"xs{xi}", xs[xi][:], xT[cb * 128:(cb + 1) * 128, tsl], [], [f"xs{xi}"])
                si_ = Sr.next()
                lc = (cb % 4) * 128
                for k in range(32):
                    _mm(P, S[si_][:], Wsl[wi][:, k, lc:lc + 128], og[:, k, :], k == 0, k == 31, [f"W{wi}", "og"], [f"S{si_}"])
                for d in deferred:
                    d()
                deferred = []
                _act(P, xs[xi][:], xs[xi][:], AF.Identity, [f"xs{xi}"], [f"xs{xi}"], scale=ALPHA)
                _stt(P, "dve", r32[:, cb, :], S[si_][:], vsb[:, cb:cb + 1], xs[xi][:], ALU.mult, ALU.add,
                     [f"S{si_}", "vsb", f"xs{xi}"], [f"r32_{cb}"])
                bi = rbr.next()
                _copy(P, "act", rb[bi][:], r32[:, cb, :], [f"r32_{cb}"], [f"rb{bi}"])
                _act(P, sqb[bi][:], r32[:, cb, :], AF.Square, [f"r32_{cb}"], [f"sqb{bi}"])

                def stat(bi=bi, cb=cb):
                    _mm(P, SUM[:], ones[:], rb[bi][:], cb == 0, cb == 31, ["ones", f"rb{bi}"], ["SUM"])
                    _mm(P, SQ[:], ones[:], sqb[bi][:], cb == 0, cb == 31, ["ones", f"sqb{bi}"], ["SQ"])
                deferred.append(stat)
            for d in deferred:
                d()
            deferred = []
            _act(P, mean[:], SUM[:], AF.Identity, ["SUM"], ["mean"], scale=1.0 / DM)
            _act(P, var[:], SQ[:], AF.Identity, ["SQ"], ["var"], scale=1.0 / DM)
            _tt(P, "dve", rstd[:], mean[:], mean[:], ALU.mult, ["mean"], ["rstd"])
            _tt(P, "dve", var[:], var[:], rstd[:], ALU.subtract, ["var", "rstd"], ["var"])
            _act(P, var[:], var[:], AF.Sqrt, ["var"], ["var"], bias=1e-5)
            P.op("dve", lambda e: e.reciprocal(out=rstd[:], in_=var[:]), reads=["var"], writes=["rstd"])
            for cb in range(32):
                ti = tr.next()
                _tt(P, "dve", tmp[ti][:], r32[:, cb, :], mean[:], ALU.subtract, [f"r32_{cb}", "mean"], [f"tmp{ti}"])
                _tt(P, "dve", tmp[ti][:], tmp[ti][:], rstd[:], ALU.mult, [f"tmp{ti}", "rstd"], [f"tmp{ti}"])
                oi = orr.next()
                _act(P, ost[oi][:], tmp[ti][:], AF.Identity, [f"tmp{ti}", "vsb"], [f"ost{oi}"],
                     scale=vsb[:, 32 + cb:33 + cb], bias=vsb[:, 64 + cb:65 + cb])
                _dma(P, stq.next(), f"st_ost{oi}", o_x[cb * 128:(cb + 1) * 128, tsl], ost[oi][:], [f"ost{oi}"], [])
        finish(P)
    return nc


NEGB = -30000.0
QT = 512


def build_L2(nq=None, do="mn", SQ=SEQ):
    nc = bass.Bass("TRN2", target_bir_lowering=False)
    S_ = SQ
    NQT = S_ // QT
    NKT = S_ // 128
    din = lambda n, s, d=BF16: nc.dram_tensor(n, list(s), d, kind="ExternalInput").ap()
    m_qn = din("m_qn", [2, 128, S_]); m_qr = din("m_qr", [2, 64, S_]); m_kn = din("m_kn", [2, 128, S_])
    m_kr = din("m_kr", [64, S_]); m_v = din("m_v", [2, S_, 128]); m_z = din("m_z", [2, S_, 128])
    n_q = din("n_q", [2, 128, S_]); n_ks = din("n_ks", [128, S_]); n_kw = din("n_kw", [128, S_])
    n_vs = din("n_vs", [S_, 128]); n_vw = din("n_vw", [S_, 128]); n_g = din("n_g", [128, 2 * (SQ // 128) * 3], F32)
    n_z = din("n_z", [2, S_, 128]); n_oc = din("n_oc", [2, S_, 128], F32); n_sel = din("n_sel", [256, S_])
    c_id = din("c_id", [128, 128]); c_cb = din("c_cb", [4, 128, 512]); c_wb = din("c_wb", [2, 8, 128, 512])
    c_nb = din("c_nb", [2, 128, 512]); c_M = din("c_M", [128, 8192]); c_al = din("c_al", [128, 2, 128], F32)
    og = nc.dram_tensor("og", [S_, 512], BF16, kind="ExternalOutput").ap()
    with ExitStack() as st:
        P = Prog(nc, st)
        bufA = P.sb("bufA", [128, S_], BF16)
        bufB = P.sb("bufB", [128, S_], BF16)
        bufC = P.sb("bufC", [128, NKT, 132], BF16)
        bufD = P.sb("bufD", [128, NKT, 132], BF16)
        ident = P.sb("ident", [128, 128], BF16)
        cb = P.sb("cb", [128, 4, 512], BF16)
        wb = P.sb("wb", [128, 16, 512], BF16)
        nb = P.sb("nb", [128, 2, 512], BF16)
        Mx = P.sb("Mx", [128, 8192], BF16)
        al = P.sb("al", [128, 2, 128], F32)
        qa = [P.sb(f"qa{i}", [128, QT], BF16) for i in range(2)]
        qb = [P.sb(f"qb{i}", [64, QT], BF16) for i in range(2)]
        pt = [P.sb(f"pt{i}", [128, QT], BF16) for i in range(3)]
        selt = [P.sb(f"selt{i}", [128, 2, QT], BF16) for i in range(2)]
        Bh = [P.sb(f"Bh{i}", [128, 2, QT], BF16) for i in range(2)]
        zt = [P.sb(f"zt{i}", [128, 4, 128], BF16) for i in range(2)]
        oct_ = [P.sb(f"oct{i}", [128, 4, 128], F32) for i in range(2)]
        gt = P.sb("gt", [128, 2, NKT, 3], F32)
        rz = P.sb("rz", [128, 8], F32)
        acc = [P.sb(f"acc{i}", [128, 128], F32) for i in range(2)]
        ost = [P.sb(f"ost{i}", [128, 128], BF16) for i in range(4)]
        Sps = [P.ps(f"S{i}", [128, 512], F32) for i in range(3)]
        Ops = [P.ps(f"O{i}", [128, 2, 132], F32) for i in range(4)]
        Sr, ptr, qr_, selr, bhr, zr, ocr, accr, ostr = (Ring(range(3)), Ring(range(3)), Ring(range(2)), Ring(range(2)),
                                                      Ring(range(2)), Ring(range(2)), Ring(range(2)), Ring(range(2)), Ring(range(4)))
        stq = Ring(["sp", "act"])
        _dma(P, "sp", "c", ident[:], c_id, [], ["ident"])
        _dma(P, "sp", "c", cb[:], c_cb.rearrange("r p t -> p r t"), [], ["cb"])
        _dma(P, "sp", "c", wb[:], c_wb.rearrange("h r p t -> p (h r) t"), [], ["wb"])
        _dma(P, "sp", "c", nb[:], c_nb.rearrange("h p t -> p h t"), [], ["nb"])
        _dma(P, "sp", "c", Mx[:], c_M, [], ["Mx"])
        _dma(P, "sp", "c", al[:], c_al, [], ["al"])
        _dma(P, "sp", "c", gt[:].rearrange("p h n b -> p (h n b)"), n_g, [], ["gt"])
        zer = P.sb("zer", [128, 264], BF16)
        P.op("dve", lambda e: e.memset(zer[:], 0.0), writes=["zer"])
        P.op("dve", lambda e: e.memset(bufC[:, :, 128:129], 1.0), writes=["bufC1"])
        P.op("dve", lambda e: e.memset(bufD[:, :, 128:129], 1.0), writes=["bufD1"])

        def load_feat(buf, name, src, rows=128):
            for c0 in range(0, S_, 4096):
                c1 = min(S_, c0 + 4096)
                _dma(P, "sp", name, buf[0:rows, c0:c1], src[:, c0:c1], [], [name])

        def load_tok(buf, name, src):
            s3 = src.rearrange("(n p) d -> p n d", p=128)
            for n0 in range(0, NKT, 8):
                n1 = min(NKT, n0 + 8)
                _dma(P, "act", name, buf[:, n0:n1, 0:128], s3[:, n0:n1, :], [], [name])

        def attn_qtile(items, Oset, Vbuf, Vnames):
            first, last = {}, {}
            for i, it in enumerate(items):
                for s in range(it["c0"] // 128, it["c1"] // 128):
                    first.setdefault(s, i)
                    last[s] = i
            pend = None
            for ob in Oset:
                _mm(P, Ops[ob][:].rearrange("p a b -> p (a b)"), zer[:, 0:128], zer[:, 0:264], True, True, ["zer"], [f"O{ob}"])

            def pv(i, it, pi):
                for s in range(it["c0"] // 128, it["c1"] // 128):
                    ob = Oset[s // 2]
                    _mm(P, Ops[ob][:, s % 2, 0:129], pt[pi][:, s * 128:(s + 1) * 128], Vbuf[:, it["kt"], 0:129],
                        False, False, [f"pt{pi}"] + Vnames, [f"O{ob}"], skip=True)

            for i, it in enumerate(items):
                si = Sr.next()
                c0, c1 = it["c0"], it["c1"]
                np_ = len(it["passes"])
                for j, (l, r, rd) in enumerate(it["passes"]):
                    _mm(P, Sps[si][:, c0:c1], l, r, j == 0, j == np_ - 1, rd, [f"S{si}"])
                if pend is not None:
                    pv(*pend)
                pi = ptr.next()
                kw = {}
                if it["bias"] is not None:
                    _act(P, pt[pi][:, c0:c1], Sps[si][:, c0:c1], AF.Exp, [f"S{si}", "al"], [f"pt{pi}"], bias=it["bias"])
                else:
                    _act(P, pt[pi][:, c0:c1], Sps[si][:, c0:c1], AF.Exp, [f"S{si}"], [f"pt{pi}"])
                pend = (i, it, pi)
            pv(*pend)

        nqt = nq or NQT
        if "m" in do:
            load_feat(bufB, "bufB", m_kr, rows=64)
            for h in range(2):
                load_feat(bufA, "bufA", m_kn[h])
                load_tok(bufC, "bufC", m_v[h])
                for qt in range(nqt):
                    qi = qr_.next()
                    tsl = slice(qt * QT, (qt + 1) * QT)
                    _dma(P, "sp", f"qa{qi}", qa[qi][:], m_qn[h][:, tsl], [], [f"qa{qi}"])
                    _dma(P, "sp", f"qb{qi}", qb[qi][:], m_qr[h][:, tsl], [], [f"qb{qi}"])
                    zi = zr.next()
                    _dma(P, "sp", f"zt{zi}", zt[zi][:], m_z[h][tsl, :].rearrange("(n p) d -> p n d", p=128), [], [f"zt{zi}"])
                    items = []
                    for kt in range(4 * qt + 4):
                        r = kt - 4 * qt
                        c0 = 128 * r if r > 0 else 0
                        ks = slice(kt * 128, (kt + 1) * 128)
                        ps = [(bufA[:, ks], qa[qi][:, c0:QT], ["bufA", f"qa{qi}"]),
                              (bufB[0:64, ks], qb[qi][:, c0:QT], ["bufB", f"qb{qi}"])]
                        if r >= 0:
                            ps.append((ident[:], cb[:, r, c0:QT], ["ident", "cb"]))
                        items.append(dict(kt=kt, c0=c0, c1=QT, passes=ps, bias=None))
                    Oset = (0, 1) if qt % 2 == 0 else (2, 3)
                    attn_qtile(items, Oset, bufC, ["bufC", "bufC1"])
                    for s in range(4):
                        ob = Oset[s // 2]
                        _ts(P, "dve", rz[:, s:s + 1], Ops[ob][:, s % 2, 128:129], 1e-30, None, ALU.max, None, [f"O{ob}"], ["rz"])
                        P.op("dve", lambda e, s=s: e.reciprocal(out=rz[:, s:s + 1], in_=rz[:, s:s + 1]), reads=["rz"], writes=["rz"])
                        oi = ostr.next()
                        _stt(P, "dve", ost[oi][:], Ops[ob][:, s % 2, 0:128], rz[:, s:s + 1], zt[zi][:, s, :], ALU.mult, ALU.mult,
                             [f"O{ob}", "rz", f"zt{zi}"], [f"ost{oi}"])
                        r0 = qt * QT + s * 128
                        _dma(P, stq.next(), f"st_ost{oi}", og[r0:r0 + 128, h * 128:(h + 1) * 128], ost[oi][:], [f"ost{oi}"], [])
        if "n" in do:
            load_feat(bufA, "bufA", n_ks)
            load_feat(bufB, "bufB", n_kw)
            load_tok(bufC, "bufC", n_vs)
            load_tok(bufD, "bufD", n_vw)
            for h in range(2):
                for qt in range(nqt):
                    qi = qr_.next()
                    tsl = slice(qt * QT, (qt + 1) * QT)
                    _dma(P, "sp", f"qa{qi}", qa[qi][:], n_q[h][:, tsl], [], [f"qa{qi}"])
                    zi = zr.next()
                    _dma(P, "sp", f"zt{zi}", zt[zi][:], n_z[h][tsl, :].rearrange("(n p) d -> p n d", p=128), [], [f"zt{zi}"])
                    oi_ = ocr.next()
                    _dma(P, "sp", f"oct{oi_}", oct_[oi_][:], n_oc[h][tsl, :].rearrange("(n p) d -> p n d", p=128), [], [f"oct{oi_}"])
                    sli = selr.next()
                    _dma(P, "sp", f"selt{sli}", selt[sli][:], n_sel[:, tsl].rearrange("(a j) t -> j a t", a=2), [], [f"selt{sli}"])
                    bi = bhr.next()
                    for a in range(2):
                        _tt(P, "pool", Bh[bi][:, a, :], selt[sli][:, a, :], nb[:, h, :], ALU.add, [f"selt{sli}", "nb"], [f"Bh{bi}"])
                    items = []
                    for kt in range(4 * qt + 4):
                        r = kt - 4 * qt
                        c0 = 128 * r if r > 0 else 0
                        ks = slice(kt * 128, (kt + 1) * 128)
                        ms = slice((kt % 64) * 128, (kt % 64 + 1) * 128)
                        ps = [(bufA[:, ks], qa[qi][:, c0:QT], ["bufA", f"qa{qi}"]),
                              (Mx[:, ms], Bh[bi][:, kt // 64, c0:QT], ["Mx", f"Bh{bi}"])]
                        if r >= 0:
                            ps.append((ident[:], cb[:, r, c0:QT], ["ident", "cb"]))
                        items.append(dict(kt=kt, c0=c0, c1=QT, passes=ps, bias=al[:, h, r + 124:r + 125]))
                    attn_qtile(items, (0, 1), bufC, ["bufC", "bufC1"])
                    items = []
                    for r in range(-4, 4):
                        kt = 4 * qt + r
                        if kt < 0:
                            continue
                        c0 = 128 * r if r > 0 else 0
                        c1 = 128 * (r + 5) if r < -1 else QT
                        ks = slice(kt * 128, (kt + 1) * 128)
                        ps = [(bufB[:, ks], qa[qi][:, c0:c1], ["bufB", f"qa{qi}"]),
                              (ident[:], wb[:, h * 8 + r + 4, c0:c1], ["ident", "wb"])]
                        items.append(dict(kt=kt, c0=c0, c1=c1, passes=ps, bias=al[:, h, r + 124:r + 125]))
                    attn_qtile(items, (2, 3), bufD, ["bufD", "bufD1"])
                    for s in range(4):
                        n = qt * 4 + s
                        osb, owb = (0, 1)[s // 2], (2, 3)[s // 2]
                        _ts(P, "dve", rz[:, 0:1], Ops[osb][:, s % 2, 128:129], 1e-30, None, ALU.max, None, [f"O{osb}"], ["rz"])
                        _ts(P, "dve", rz[:, 1:2], Ops[owb][:, s % 2, 128:129], 1e-30, None, ALU.max, None, [f"O{owb}"], ["rz"])
                        P.op("dve", lambda e: e.reciprocal(out=rz[:, 2:4], in_=rz[:, 0:2]), reads=["rz"], writes=["rz"])
                        _tt(P, "dve", rz[:, 4:6], rz[:, 2:4], gt[:, h, n, 1:3], ALU.mult, ["rz", "gt"], ["rz"])
                        ai = accr.next()
                        _ts(P, "dve", acc[ai][:], oct_[oi_][:, s, :], gt[:, h, n, 0:1], None, ALU.mult, None, [f"oct{oi_}", "gt"], [f"acc{ai}"])
                        _stt(P, "dve", acc[ai][:], Ops[osb][:, s % 2, 0:128], rz[:, 4:5], acc[ai][:], ALU.mult, ALU.add,
                             [f"O{osb}", "rz", f"acc{ai}"], [f"acc{ai}"])
                        _stt(P, "dve", acc[ai][:], Ops[owb][:, s % 2, 0:128], rz[:, 5:6], acc[ai][:], ALU.mult, ALU.add,
                             [f"O{owb}", "rz", f"acc{ai}"], [f"acc{ai}"])
                        oi = ostr.next()
                        _tt(P, "dve", ost[oi][:], acc[ai][:], zt[zi][:, s, :], ALU.mult, [f"acc{ai}", f"zt{zi}"], [f"ost{oi}"])
                        r0 = qt * QT + s * 128
                        _dma(P, stq.next(), f"st_ost{oi}", og[r0:r0 + 128, 256 + h * 128:256 + (h + 1) * 128], ost[oi][:], [f"ost{oi}"], [])
        finish(P)
    return nc


NT2A = 32


def build_L2a(TI, ntiles=None, SQ=SEQ, tiles=None):
    nc = bass.Bass("TRN2", target_bir_lowering=False)
    S_ = SQ
    NCMP = S_ // 16
    tiles = tiles if tiles is not None else list(range(len(TI)))
    din = lambda n, s, d=BF16: nc.dram_tensor(n, list(s), d, kind="ExternalInput").ap()
    kcs = din("kcs", [128, S_]); vcs = din("vcs", [128, S_])
    posT = din("posT", [128, 64], F32)
    w1 = din("w1", [128, 2, 32, 128], F32)
    w2 = din("w2", [128, 2, 128], F32)
    q8 = din("q8", [8, 128, NT2A * 128])
    c_id = din("c_id", [128, 128]); c_R = din("c_R", [3, 8, 1024]); c_dm = din("c_dm", [128, 32]); c_c0 = din("c_c0", [128, 32], F32)
    c_bt = din("c_bt", [128, 8], F32); c_FB = din("c_FB", [128, 512], F32)
    tmeta = None
    o_sel = nc.dram_tensor("o_sel", [NT2A * 128, 256], BF16, kind="ExternalOutput").ap()
    o_oc = nc.dram_tensor("o_oc", [8, NT2A * 128, 128], F32, kind="ExternalOutput").ap()
    with ExitStack() as st:
        P = Prog(nc, st)
        kc_sb = P.sb("kc_sb", [128, S_], BF16)
        vc_sb = P.sb("vc_sb", [128, S_], BF16)
        w1_sb = P.sb("w1_sb", [128, 2, 32, 128], BF16)
        w2_sb = P.sb("w2_sb", [128, 2, 128], BF16)
        pos_sb = P.sb("pos_sb", [128, 64], BF16)
        pb = P.sb("pb", [128, 2], F32)
        hid = P.sb("hid", [128, 2, NCMP], BF16)
        KcT = P.sb("KcT", [128, NCMP], BF16)
        Vc = P.sb("Vc", [128, NCMP // 128, 128], BF16)
        ident = P.sb("ident", [128, 128], BF16)
        Rt = P.sb("Rt", [3, 8, 1024], BF16)
        dm = P.sb("dm", [128, 32], BF16)
        c0a = P.sb("c0a", [128, 32], F32)
        bt = P.sb("bt", [128, 8], F32)
        FB = P.sb("FB", [128, 512], F32)
        ones3 = P.sb("ones3", [3, 128], BF16)
        qs = [P.sb(f"qs{i}", [128, 8, 128], BF16) for i in range(2)]
        E = [P.sb(f"E{i}", [128, 1024], BF16) for i in range(2)]
        ET = [P.sb(f"ET{i}", [128, 4, 128], BF16) for i in range(2)]
        Z = P.sb("Z", [128, 4], F32)
        Pacc = P.sb("Pacc", [128, 1032], F32)
        imp = P.sb("imp", [128, 256], F32)
        wk = P.sb("wk", [128, 256], F32)
        m16 = P.sb("m16", [128, 16], F32)
        selo = [P.sb(f"selo{i}", [128, 256], BF16) for i in range(2)]
        oco = [P.sb(f"oco{i}", [128, 128], F32) for i in range(3)]
        Sp = [P.ps(f"S{i}", [128, 1024], F32) for i in range(2)]
        TP = [P.ps(f"TP{i}", [128, 4, 128], BF16) for i in range(2)]
        Op = [P.ps(f"O{i}", [128, 128], F32) for i in range(2)]
        Sr, Er, ETr, TPr, Or, qr_, selr, ocr = (Ring(range(2)), Ring(range(2)), Ring(range(2)), Ring(range(2)),
                                               Ring(range(2)), Ring(range(2)), Ring(range(2)), Ring(range(3)))
        stq = Ring(["sp", "act"])
        _dma(P, "sp", "c", ident[:], c_id, [], ["ident"])
        _dma(P, "sp", "c", Rt[:], c_R, [], ["Rt"])
        _dma(P, "sp", "c", dm[:], c_dm, [], ["dm"])
        _dma(P, "sp", "c", c0a[:], c_c0, [], ["c0a"])
        _dma(P, "sp", "c", bt[:], c_bt, [], ["bt"])
        _dma(P, "sp", "c", FB[:], c_FB, [], ["FB"])
        _dma(P, "pool", "cp", w1_sb[:], w1, [], ["w1_sb"])
        _dma(P, "pool", "cp", w2_sb[:], w2, [], ["w2_sb"])
        _dma(P, "pool", "cp", pos_sb[:], posT, [], ["pos_sb"])
        for c0 in range(0, S_, 4096):
            _dma(P, "sp", "kc", kc_sb[:, c0:c0 + 4096], kcs[:, c0:c0 + 4096], [], ["kc_sb"])
            _dma(P, "act", "vc", vc_sb[:, c0:c0 + 4096], vcs[:, c0:c0 + 4096], [], ["vc_sb"])
        P.op("dve", lambda e: e.memset(ones3[:], 1.0), writes=["ones3"])
        P.op("dve", lambda e: e.memset(hid[:], 0.0), writes=["hid"])
        for kv, (src, sname) in enumerate(((kc_sb, "kc_sb"), (vc_sb, "vc_sb"))):
            v3 = src[:].rearrange("p (n s) -> p n s", s=16)
            for L in range(32):
                _mm(P, Op[0][:, kv:kv + 1], w1_sb[:, kv, L, :], pos_sb[:, kv * 32 + L:kv * 32 + L + 1], L == 0, L == 31,
                    ["w1_sb", "pos_sb"], ["O0"])
            _copy(P, "dve", pb[:, kv:kv + 1], Op[0][:, kv:kv + 1], ["O0"], ["pb"])
            NV = NCMP - 1
            for n0 in range(0, NV, 512):
                cnt = min(512, NV - n0)
                si = Sr.next()
                for L in range(32):
                    rhs = v3[:, n0:n0 + cnt, L] if L < 16 else v3[:, n0 + 1:n0 + 1 + cnt, L - 16]
                    _mm(P, Sp[si][:, 0:cnt], w1_sb[:, kv, L, :], rhs, L == 0, L == 31, ["w1_sb", sname], [f"S{si}"])
                _act(P, hid[:, kv, n0:n0 + cnt], Sp[si][:, 0:cnt], AF.Silu, [f"S{si}", "pb"], ["hid"], bias=pb[:, kv:kv + 1])
        for n0 in range(0, NCMP, 512):
            si = Sr.next()
            _mm(P, Sp[si][:, 0:512], w2_sb[:, 0, :], hid[:, 0, n0:n0 + 512], True, True, ["w2_sb", "hid"], [f"S{si}"])
            _copy(P, "dve", KcT[:, n0:n0 + 512], Sp[si][:, 0:512], [f"S{si}"], ["KcT"])
        for c in range(NCMP // 128):
            oi = Or.next()
            _mm(P, Op[oi][:], hid[:, 1, c * 128:(c + 1) * 128], w2_sb[:, 1, :], True, True, ["w2_sb", "hid"], [f"O{oi}"])
            _copy(P, "dve", Vc[:, c, :], Op[oi][:], [f"O{oi}"], ["Vc"])
        for li in (tiles[:ntiles] if ntiles else tiles):
            qi = qr_.next()
            _dma(P, "sp", f"qs{qi}", qs[qi][:], q8[:, :, li * 128:(li + 1) * 128].rearrange("h d t -> d h t"), [], [f"qs{qi}"])
            P.op("pool", lambda e: e.memset(Pacc[:], 0.0), writes=["Pacc"])
            for hh in range(8):
                si = Sr.next()
                ei = Er.next()
                ti = TI[li]
                ncols = 8 * ti + 7
                chunks = [(c0, min(512, ncols - c0)) for c0 in range(0, ncols, 512)]
                for ci, (c0, w) in enumerate(chunks):
                    lastc = ci == len(chunks) - 1
                    _mm(P, Sp[si][:, c0:c0 + w], qs[qi][:, hh, :], KcT[:, c0:c0 + w], True, False, [f"qs{qi}", "KcT"], [f"S{si}"])
                    r0 = 1016 - 8 * ti + c0
                    _mm(P, Sp[si][:, c0:c0 + w], ones3[:], Rt[:, hh, r0:r0 + w], False, not lastc, ["ones3", "Rt"], [f"S{si}"])
                    if lastc:
                        nm = min(32, ncols)
                        _mm(P, Sp[si][:, ncols - nm:ncols], ident[:], dm[:, 32 - nm:32], False, True, ["ident", "dm"], [f"S{si}"], skip=True)
                for ci, (c0, w) in enumerate(chunks):
                    _act(P, E[ei][:, c0:c0 + w], Sp[si][:, c0:c0 + w], AF.Exp, [f"S{si}", "bt"], [f"E{ei}"],
                         bias=bt[:, hh:hh + 1], accum_out=Z[:, ci:ci + 1])
                if len(chunks) == 2:
                    _tt(P, "dve", Z[:, 0:1], Z[:, 0:1], Z[:, 1:2], ALU.add, [f"E{ei}"], ["Z"])
                _ts(P, "dve", Z[:, 2:3], Z[:, 0:1], 1e-30, None, ALU.max, None, [f"E{ei}", "Z"], ["Z"])
                P.op("dve", lambda e: e.reciprocal(out=Z[:, 3:4], in_=Z[:, 2:3]), reads=["Z"], writes=["Z"])
                _stt(P, "dve", Pacc[:, 0:ncols], E[ei][:, 0:ncols], Z[:, 3:4], Pacc[:, 0:ncols], ALU.mult, ALU.add,
                     [f"E{ei}", "Z", "Pacc"], ["Pacc"])
                nch = (ncols + 127) // 128
                oi = Or.next()
                pend = []
                for g0 in range(0, nch, 4):
                    tpi = TPr.next()
                    ws = []
                    for c in range(g0, min(nch, g0 + 4)):
                        w = min(128, ncols - c * 128)
                        ws.append((c, w))
                        _tr(P, TP[tpi][0:w, c - g0, :], E[ei][:, c * 128:c * 128 + w], ident[:], [f"E{ei}", "ident"], [f"TP{tpi}"])
                    for d in pend:
                        d()
                    pend = []
                    eti = ETr.next()
                    _copy(P, "act" if (g0 // 4) % 2 else "dve", ET[eti][:], TP[tpi][:], [f"TP{tpi}"], [f"ET{eti}"])

                    def pvs(ws=ws, eti=eti, g0=g0, oi=oi, nch=nch):
                        for (c, w) in ws:
                            _mm(P, Op[oi][:], ET[eti][0:w, c - g0, :], Vc[0:w, c, :], c == 0, c == nch - 1, [f"ET{eti}", "Vc"], [f"O{oi}"])
                    pend.append(pvs)
                for d in pend:
                    d()
                oo = ocr.next()
                _ts(P, "dve", oco[oo][:], Op[oi][:], Z[:, 3:4], None, ALU.mult, None, [f"O{oi}", "Z"], [f"oco{oo}"])
                _dma(P, stq.next(), f"st_oco{oo}", o_oc[hh][li * 128:(li + 1) * 128, :], oco[oo][:], [f"oco{oo}"], [])
            P4 = Pacc[:, 0:1024].rearrange("p (j f) -> p j f", f=4)
            P.op("dve", lambda e, P4=P4: e.tensor_reduce(out=imp[:], in_=P4, axis=AX.X, op=ALU.add), reads=["Pacc"], writes=["imp"])
            _tt(P, "dve", imp[:, 1:256], imp[:, 1:256], P4[:, 0:255, 3], ALU.add, ["imp", "Pacc"], ["imp"])
            f0 = 254 - 2 * ti
            _tt(P, "dve", imp[:], imp[:], FB[:, f0:f0 + 256], ALU.add, ["imp", "FB"], ["imp"])
            _tt(P, "dve", imp[:, 0:1], imp[:, 0:1], c0a[:, li:li + 1], ALU.add, ["imp", "c0a"], ["imp"])
            P.op("dve", lambda e: e.max(out=m16[:, 0:8], in_=imp[:]), reads=["imp"], writes=["m16"])
            P.op("dve", lambda e: e.match_replace(out=wk[:], in_to_replace=m16[:, 0:8], in_values=imp[:], imm_value=-3e38),
                 reads=["imp", "m16"], writes=["wk"])
            P.op("dve", lambda e: e.max(out=m16[:, 8:16], in_=wk[:]), reads=["wk"], writes=["m16"])
            so = selr.next()
            _ts(P, "dve", selo[so][:], imp[:], m16[:, 15:16], NEGB, ALU.is_lt, ALU.mult, ["imp", "m16"], [f"selo{so}"])
            _dma(P, stq.next(), f"st_selo{so}", o_sel[li * 128:(li + 1) * 128, :], selo[so][:], [f"selo{so}"], [])
        finish(P)
    return nc


def _run(nc, in_maps):
    res = run_bass_kernel_spmd(nc, in_maps, core_ids=list(range(len(in_maps))))
    return res.results


def feat_layout(v):
    return np.ascontiguousarray(v.reshape(-1, 128).T)


def run_L0(c, w_ada, b_ada):
    nc = build_L0()
    cT = feat_layout(c[0])
    maps = []
    for i in range(NCORE):
        wa = np.ascontiguousarray(w_ada[:, :, i * 1536:(i + 1) * 1536])
        ba = np.concatenate([feat_layout(b_ada[l, i * 1536:(i + 1) * 1536]) for l in range(2)], axis=1)
        maps.append(dict(cT=cT, wada=wa, bada=np.ascontiguousarray(ba)))
    res = _run(nc, maps)
    mod = np.zeros((2, 12288), np.float32)
    for i in range(NCORE):
        m = res[i]["mod"]
        for l in range(2):
            mod[l, i * 1536:(i + 1) * 1536] = m[:, l * 12:(l + 1) * 12].T.reshape(-1)
    return mod


KV0 = 5440


def kvblk(w, idx):
    return w[:, KV0 + idx * 128: KV0 + (idx + 1) * 128]


def prep_L1_weights(w_in_l, w_q_up_l, w_kv_up_l, qn_l, kvn_l):
    wf = np.concatenate([w_in_l[:, 0:1280], w_in_l[:, 3392:5440]] + [kvblk(w_in_l, i) for i in (0, 1, 2, 3, 4, 5, 8, 9)]
                        + [w_in_l[:, 1280:1344]], axis=1)
    wt = np.concatenate([w_in_l[:, 1344:3392], w_in_l[:, 7024:9072]] + [kvblk(w_in_l, i) for i in (6, 7, 10, 11)]
                        + [w_in_l[:, 6976:7024]], axis=1)
    wkv4 = w_kv_up_l.reshape(512, 16, 2, 128)
    wkv = np.concatenate([wkv4[:, :, 0, :].reshape(512, 2048), wkv4[:, :, 1, :].reshape(512, 2048)], axis=1)
    return dict(wf=np.ascontiguousarray(wf), wt=np.ascontiguousarray(wt), wq=np.ascontiguousarray(w_q_up_l),
                wkv=np.ascontiguousarray(wkv), qg=feat_layout(qn_l), kvg=feat_layout(kvn_l))


def rope_consts():
    inv = (np.float32(10000.0) ** (-np.arange(0, 64, 2, dtype=np.float32) / np.float32(64))).astype(np.float32)
    invf = np.concatenate([inv, inv])[:, None].astype(np.float32)
    rm = np.zeros((64, 64), np.float32)
    for m in range(32):
        rm[m + 32, m] = -1.0
    for m in range(32, 64):
        rm[m - 32, m] = 1.0
    return invf, rm


def run_L1(xT_shards, mod_l, positions, wd, ncores=NCORE, **bk):
    nc = build_L1(**bk)
    invf, rm = rope_consts()
    modsb = np.concatenate([feat_layout(mod_l[0:4096]), feat_layout(mod_l[4096:8192])], axis=1)
    maps = []
    for i in range(ncores):
        d = dict(wd)
        d.update(xT=xT_shards[i], mod=np.ascontiguousarray(modsb),
                 pos=np.ascontiguousarray(positions[:, i * TSH:(i + 1) * TSH]).astype(np.int32), invf=invf, rmat=rm)
        maps.append(d)
    return _run(nc, maps)


def run_L3(ogT_shards, xT_shards, w_out_l, gate, ln_g, ln_b, ncores=NCORE, **bk):
    nc = build_L3(**bk)
    vec = np.ascontiguousarray(np.concatenate([feat_layout(gate), feat_layout(ln_g), feat_layout(ln_b)], axis=1))
    wo = np.ascontiguousarray(w_out_l)
    maps = [dict(ogT=ogT_shards[i], xT=xT_shards[i], wo=wo, vec=vec) for i in range(ncores)]
    return _run(nc, maps)


def nsa_slopes():
    return (2.0 ** (-8.0 * np.arange(1, 17, dtype=np.float64) / 16.0))


def L2_consts(core):
    sl = nsa_slopes()[[2 * core, 2 * core + 1]]
    p = np.arange(128)[:, None]
    t = np.arange(512)[None, :]
    ident = np.eye(128, dtype=np.float32).astype(NPBF)
    cbm = np.stack([np.where(t >= 128 * r + p, 0.0, NEGB) for r in range(4)]).astype(np.float32).astype(NPBF)
    nbv = np.stack([np.broadcast_to((-s_ * t).astype(np.float32), (128, 512)) for s_ in sl]).astype(NPBF)
    wbm = np.zeros((2, 8, 128, 512), np.float32)
    for hh in range(2):
        for ri in range(8):
            r = ri - 4
            d = t - p - 128 * r
            valid = (d >= 0) & (d <= 511)
            wbm[hh, ri] = np.where(valid, nbv[hh].astype(np.float32), NEGB)
    M = (np.arange(8192)[None, :] // 64 == np.arange(128)[:, None]).astype(np.float32).astype(NPBF)
    al = np.zeros((128, 2, 128), np.float32)
    for hh in range(2):
        for i in range(128):
            al[:, hh, i] = (sl[hh] * (np.arange(128) + 128.0 * (i - 124))).astype(np.float32)
    return dict(c_id=ident, c_cb=cbm, c_wb=wbm.astype(NPBF), c_nb=nbv, c_M=M, c_al=al)


def g_layout(g2):
    S_ = g2.shape[1]
    return np.ascontiguousarray(g2.reshape(2, S_ // 128, 128, 3).transpose(2, 0, 1, 3).reshape(128, -1))


TI2A = [4 * i + 3 for i in range(32)]


def L2a_consts(core):
    g, r = core // 4, core % 4
    sl = nsa_slopes()[g * 8:(g + 1) * 8]
    p = np.arange(128)[:, None]
    x = np.arange(1024)
    R = np.zeros((3, 8, 1024), np.float32)
    for hh in range(8):
        v = (sl[hh] * (16.0 * (x - 992 - 8 * r) + 31.0)).astype(np.float32)
        hi = v.astype(NPBF).astype(np.float32)
        mid = (v - hi).astype(NPBF).astype(np.float32)
        lo = (v - hi - mid).astype(NPBF).astype(np.float32)
        R[0, hh], R[1, hh], R[2, hh] = hi, mid, lo
    m = np.arange(32)[None, :] - 1 - 8 * r
    dm = np.where(p >= 16 * m + 31, 0.0, NEGB).astype(np.float32)
    bt = (-(sl[None, :]) * p).astype(np.float32)
    xx = np.arange(512)[None, :]
    jrel = xx - 248 - 2 * r
    cur = (p >= 64).astype(np.int64)
    FB = np.where(jrel > cur, -1e30, np.where((jrel == cur) | (jrel == cur - 1), 1e4, 0.0)).astype(np.float32)
    c0 = np.zeros((128, 32), np.float32)
    for i in range(32):
        c0[:, i] = 1e4 if (4 * i + r) >= 1 else 0.0
    return dict(c_id=np.eye(128, dtype=np.float32).astype(NPBF), c_R=R.astype(NPBF), c_dm=dm.astype(NPBF), c_bt=bt, c_FB=FB, c_c0=c0)


def L2a_weights(cmp_pos_l, w_cmp1_l, w_cmp2_l):
    posT = np.ascontiguousarray(cmp_pos_l.transpose(2, 0, 1).reshape(128, 64))
    w1 = np.ascontiguousarray(w_cmp1_l.reshape(2, 32, 128, 128).transpose(2, 0, 1, 3))
    w2 = np.ascontiguousarray(w_cmp2_l.transpose(1, 0, 2))
    return dict(posT=posT, w1=w1, w2=w2)


def _cat_tok(res, key, idx=None, axis=-1):
    if idx is None:
        return np.concatenate([res[i][key] for i in range(NCORE)], axis=axis)
    return np.concatenate([res[i][key][idx] for i in range(NCORE)], axis=axis)


def _layer(l, xT_sh, mod, inp):
    positions = inp["positions"]
    wd = prep_L1_weights(inp["w_in"][l], inp["w_q_up"][l], inp["w_kv_up"][l], inp["mla_q_norm"][l], inp["mla_kv_norm"][l])
    r1 = run_L1(xT_sh, mod[l], positions, wd)
    del wd
    S_ = SEQ
    kp = [_cat_tok(r1, "o_kp", i) for i in range(8)]
    qs = [_cat_tok(r1, "o_qs", h) for h in range(16)]
    w2a = L2a_weights(inp["cmp_pos"][l], inp["w_cmp1"][l], inp["w_cmp2"][l])
    maps = []
    for c in range(NCORE):
        g, r = c // 4, c % 4
        q8 = np.stack([qs[g * 8 + hh].reshape(128, S_ // 128, 128)[:, r::4, :].reshape(128, -1) for hh in range(8)])
        d = dict(kcs=kp[g], vcs=kp[2 + g], q8=np.ascontiguousarray(q8))
        d.update(w2a)
        d.update(L2a_consts(c))
        maps.append(d)
    nc2a = build_L2a(TI2A)
    r2a = _run(nc2a, maps)
    del maps
    selT = []
    oc_full = np.zeros((16, S_, 128), np.float32)
    for g in range(2):
        sel = np.zeros((S_ // 128, 128, 256), NPBF)
        for r in range(4):
            c = g * 4 + r
            sel[r::4] = np.asarray(r2a[c]["o_sel"]).reshape(32, 128, 256)
            oc = np.asarray(r2a[c]["o_oc"]).reshape(8, 32, 128, 128)
            oc_full[g * 8:(g + 1) * 8].reshape(8, S_ // 128, 128, 128)[:, r::4] = oc
        selT.append(np.ascontiguousarray(sel.reshape(S_, 256).T))
    del r2a
    v_full = _cat_tok(r1, "o_v", axis=0)
    z_full = _cat_tok(r1, "o_z", axis=0)
    vp_full = _cat_tok(r1, "o_vp", axis=0)
    g_full = _cat_tok(r1, "o_g", axis=0)
    kr_full = _cat_tok(r1, "o_kr")
    maps = []
    for c in range(NCORE):
        g = c // 4
        hs = (2 * c, 2 * c + 1)
        d = dict(
            m_qn=np.stack([_cat_tok(r1, "o_qn", h) for h in hs]),
            m_qr=np.stack([_cat_tok(r1, "o_qr", h) for h in hs]),
            m_kn=np.stack([_cat_tok(r1, "o_kn", h) for h in hs]),
            m_kr=kr_full,
            m_v=np.stack([np.ascontiguousarray(v_full[:, h * 128:(h + 1) * 128]) for h in hs]),
            m_z=np.stack([np.ascontiguousarray(z_full[:, h * 128:(h + 1) * 128]) for h in hs]),
            n_q=np.stack([qs[h] for h in hs]),
            n_ks=kp[4 + g], n_kw=kp[6 + g],
            n_vs=np.ascontiguousarray(vp_full[:, g * 128:(g + 1) * 128]),
            n_vw=np.ascontiguousarray(vp_full[:, 256 + g * 128:256 + (g + 1) * 128]),
            n_g=g_layout(np.stack([np.stack([g_full[:, b * 16 + h] for b in range(3)], axis=-1) for h in hs])),
            n_z=np.stack([np.ascontiguousarray(z_full[:, 2048 + h * 128:2048 + (h + 1) * 128]) for h in hs]),
            n_oc=np.ascontiguousarray(oc_full[[hs[0], hs[1]]]),
            n_sel=selT[g],
        )
        d.update(L2_consts(c))
        maps.append(d)
    del r1, v_full, z_full, vp_full, g_full, kp, qs
    nc2 = build_L2()
    r2 = _run(nc2, maps)
    del maps
    og_full = np.zeros((S_, 4096), NPBF)
    for c in range(NCORE):
        o = np.asarray(r2[c]["og"])
        og_full[:, 2 * c * 128:(2 * c + 2) * 128] = o[:, 0:256]
        og_full[:, 2048 + 2 * c * 128:2048 + (2 * c + 2) * 128] = o[:, 256:512]
    del r2
    ogT_sh = [np.ascontiguousarray(og_full[i * TSH:(i + 1) * TSH].T) for i in range(NCORE)]
    del og_full
    r3 = run_L3(ogT_sh, xT_sh, inp["w_out"][l], mod[l][8192:12288], inp["ln_g"][l], inp["ln_b"][l])
    return [np.asarray(r3[i]["o_x"]) for i in range(NCORE)]


def kernel(**inputs):
    inp = {k: np.asarray(v) for k, v in inputs.items()}
    x = inp["x"][0]
    mod = run_L0(inp["c"], inp["w_ada"], inp["b_ada"])
    xT_sh = [np.ascontiguousarray(x[i * TSH:(i + 1) * TSH].T) for i in range(NCORE)]
    for l in range(2):
        xT_sh = _layer(l, xT_sh, mod, inp)
    out = np.concatenate([s.T for s in xT_sh], axis=0)[None]
    return np.ascontiguousarray(out.astype(np.float32))
```
